# Optimizing a Trainium2 kernel written in Bass

```python
import math
import jax, jax.numpy as jnp
from jax import lax
import numpy as np

D_MODEL = 2048
BATCH = 2
SEQ = 8192
DEPTH = 1

DIFF_HEADS = 8
DIFF_D = 64
NSA_HEADS = 8
NSA_KV = 2
NSA_GROUP = NSA_HEADS // NSA_KV
NSA_DK = 128
NSA_DV = 128
CMP_LEN = 32
CMP_STRIDE = 16
CMP_HIDDEN = 256
SLC_LEN = 64
SLC_TOPK = 16
WINDOW = 512
N_BUCKETS = 32
MAX_DISTANCE = 128
D_FF = 5632
CONV_W = 3
Q_BLOCK = 128
EPS = 1e-6
NEG = -1e30
FORCE = 1e30
TINY = 1e-30

DIFF_QK = DIFF_HEADS * 2 * DIFF_D
DIFF_V = DIFF_HEADS * 2 * DIFF_D
NSA_Q = NSA_HEADS * NSA_DK
NSA_K = NSA_KV * NSA_DK
NSA_V = NSA_KV * NSA_DV
N_GATES = NSA_HEADS * 3
SPLIT_SIZES = (DIFF_QK, DIFF_QK, DIFF_V, NSA_Q, NSA_K, NSA_V, NSA_K, NSA_V, NSA_K, NSA_V, N_GATES)
IN_COLS = 2 * DIFF_QK + DIFF_V + NSA_Q + 3 * (NSA_K + NSA_V) + N_GATES
MIX_WIDTH = DIFF_HEADS * 2 * DIFF_D + NSA_HEADS * NSA_DV

kernel_name = 'hybrid_diffattn_nsa_convglu_block'


def rms_norm(x, g):
    xf = x.astype(jnp.float32)
    y = xf * lax.rsqrt(jnp.mean(xf * xf, axis=-1, keepdims=True) + EPS)
    return (y * g.astype(jnp.float32)).astype(x.dtype)


def rel_bucket(delta):
    n = jnp.maximum(delta, 0)
    max_exact = N_BUCKETS // 2
    nf = jnp.maximum(n, 1).astype(jnp.float32)
    large = max_exact + (jnp.log(nf / max_exact) / math.log(MAX_DISTANCE / max_exact)
                         * (N_BUCKETS - max_exact)).astype(jnp.int32)
    large = jnp.minimum(large, N_BUCKETS - 1)
    return jnp.where(n < max_exact, n, large)


def masked_softmax(s, mask):
    s = jnp.where(mask, s, NEG)
    m = jnp.max(s, axis=-1, keepdims=True)
    p = jnp.exp(s - m) * mask
    return p / jnp.maximum(jnp.sum(p, axis=-1, keepdims=True), TINY)


def selection_map(nb, nc):
    r = SLC_LEN // CMP_STRIDE
    j = np.arange(nb)[:, None, None]
    m = np.arange(r)[None, :, None]
    n = np.arange(CMP_LEN // CMP_STRIDE)[None, None, :]
    c = r * j - m - n
    jj = np.broadcast_to(j, c.shape)
    valid = (c >= 0) & (c < nc)
    out = np.zeros((nb, nc), np.float32)
    np.add.at(out, (jj[valid], c[valid]), 1.0)
    return out


def compress(k, pos_emb, w1, w2):
    B, S, G, d = k.shape
    nc = (S - CMP_LEN) // CMP_STRIDE + 1
    idx = jnp.arange(nc)[:, None] * CMP_STRIDE + jnp.arange(CMP_LEN)[None, :]
    blocks = k[:, idx] + pos_emb[:, None, :]
    blocks = jnp.transpose(blocks, (0, 3, 1, 2, 4)).reshape(B, G, nc, CMP_LEN * d)
    return jax.nn.gelu(blocks @ w1, approximate=True) @ w2


def diff_attention(q, k, v, lam, lam_init, subln, bias_tab):
    B, H, S, _, d = q.shape
    nblk = S // Q_BLOCK
    kpos = jnp.arange(S)
    scale = d ** -0.5

    def block(i):
        s0 = i * Q_BLOCK
        qb = lax.dynamic_slice_in_dim(q, s0, Q_BLOCK, axis=2)
        qpos = s0 + jnp.arange(Q_BLOCK)
        delta = qpos[:, None] - kpos[None, :]
        bias = jnp.transpose(bias_tab[rel_bucket(delta)], (2, 0, 1))
        logits = jnp.einsum('bhqmd,bhkmd->bhmqk', qb, k).astype(jnp.float32) * scale + bias[None, :, None]
        p = masked_softmax(logits, (delta >= 0)[None, None, None])
        a = p[:, :, 0] - lam * p[:, :, 1]
        return jnp.einsum('bhqk,bhke->bhqe', a.astype(v.dtype), v)

    o = lax.map(block, jnp.arange(nblk))
    o = jnp.transpose(o, (1, 2, 0, 3, 4)).reshape(B, H, S, 2 * d)
    return rms_norm(o, subln) * (1.0 - lam_init)


def nsa_attention(q, k_cmp, v_cmp, k_slc, v_slc, k_win, v_win, gates, bias_nsa):
    B, G, Hg, S, dk = q.shape
    nc = k_cmp.shape[2]
    nb = S // SLC_LEN
    topk = min(SLC_TOPK, nb)
    nsel = topk * SLC_LEN
    scale = dk ** -0.5
    sel_map = jnp.asarray(selection_map(nb, nc))
    cmp_end = jnp.arange(nc) * CMP_STRIDE + (CMP_LEN - 1)
    ks_blk = k_slc.reshape(B, G, nb, SLC_LEN, dk)
    vs_blk = v_slc.reshape(B, G, nb, SLC_LEN, -1)
    pad = ((0, 0), (0, 0), (WINDOW, 0), (0, 0))
    kw_pad = jnp.pad(k_win, pad)
    vw_pad = jnp.pad(v_win, pad)
    table_g = jnp.transpose(bias_nsa, (1, 0, 2))
    bi = jnp.arange(B)[:, None, None, None]
    gi = jnp.arange(G)[None, :, None, None]
    blk = jnp.arange(nb)

    def block(i):
        s0 = i * Q_BLOCK
        qb = lax.dynamic_slice_in_dim(q, s0, Q_BLOCK, axis=3)
        gb = lax.dynamic_slice_in_dim(gates, s0, Q_BLOCK, axis=3)
        qpos = s0 + jnp.arange(Q_BLOCK)
        lc = jnp.einsum('bghqd,bgcd->bghqc', qb, k_cmp).astype(jnp.float32) * scale
        pc = masked_softmax(lc, cmp_end[None, :] <= qpos[:, None])
        o_cmp = jnp.einsum('bghqc,bgcd->bghqd', pc.astype(v_cmp.dtype), v_cmp)
        imp = jnp.einsum('bgqc,jc->bgqj', jnp.sum(pc, axis=2), sel_map)
        cur = (qpos // SLC_LEN)[:, None]
        forced = (blk[None] == 0) | (blk[None] == cur) | (blk[None] == cur - 1)
        imp = jnp.where(forced, FORCE, jnp.where(blk[None] <= cur, imp, NEG))
        _, idx = lax.top_k(imp, topk)
        k_sel = ks_blk[bi, gi, idx].reshape(B, G, Q_BLOCK, nsel, dk)
        v_sel = vs_blk[bi, gi, idx].reshape(B, G, Q_BLOCK, nsel, -1)
        kpos_sel = (idx[..., None] * SLC_LEN + jnp.arange(SLC_LEN)).reshape(B, G, Q_BLOCK, nsel)
        d_sel = qpos[:, None] - kpos_sel
        b_sel = jnp.transpose(table_g[gi, rel_bucket(d_sel)], (0, 1, 4, 2, 3))
        ls = jnp.einsum('bghqd,bgqkd->bghqk', qb, k_sel).astype(jnp.float32) * scale + b_sel
        ps = masked_softmax(ls, (d_sel >= 0)[:, :, None])
        o_slc = jnp.einsum('bghqk,bgqkd->bghqd', ps.astype(v_sel.dtype), v_sel)
        kwb = lax.dynamic_slice_in_dim(kw_pad, s0, Q_BLOCK + WINDOW, axis=2)
        vwb = lax.dynamic_slice_in_dim(vw_pad, s0, Q_BLOCK + WINDOW, axis=2)
        kpos_w = s0 - WINDOW + jnp.arange(Q_BLOCK + WINDOW)
        d_w = qpos[:, None] - kpos_w[None, :]
        wmask = (d_w >= 0) & (d_w < WINDOW) & (kpos_w[None, :] >= 0)
        b_w = jnp.transpose(bias_nsa[rel_bucket(d_w)], (2, 3, 0, 1))
        lw = jnp.einsum('bghqd,bgkd->bghqk', qb, kwb).astype(jnp.float32) * scale + b_w
        pw = masked_softmax(lw, wmask)
        o_win = jnp.einsum('bghqk,bgkd->bghqd', pw.astype(vwb.dtype), vwb)
        return gb[..., 0:1] * o_cmp + gb[..., 1:2] * o_slc + gb[..., 2:3] * o_win

    o = lax.map(block, jnp.arange(S // Q_BLOCK))
    return jnp.transpose(o, (1, 2, 3, 0, 4, 5)).reshape(B, G, Hg, S, -1)


def causal_dwconv(u, w, b):
    S = u.shape[1]
    up = jnp.pad(u, ((0, 0), (CONV_W - 1, 0), (0, 0)))
    y = b
    for j in range(CONV_W):
        y = y + w[j] * up[:, j:j + S]
    return y


def setup_inputs(seed: int = 0) -> dict:
    key = jax.random.key(seed)
    ks = jax.random.split(key, 24)
    f32 = jnp.float32

    def nrm(k, shape, scale):
        return jax.random.normal(k, shape, f32) * scale

    def gain(k, shape):
        return 1.0 + 0.05 * jax.random.normal(k, shape, f32)

    return {
        'x': nrm(ks[0], (BATCH, SEQ, D_MODEL), 1.0),
        'pre_mix_norm': gain(ks[1], (DEPTH, D_MODEL)),
        'w_in': nrm(ks[2], (DEPTH, D_MODEL, IN_COLS), D_MODEL ** -0.5),
        'lambda_q1': nrm(ks[3], (DEPTH, DIFF_D), 0.1),
        'lambda_k1': nrm(ks[4], (DEPTH, DIFF_D), 0.1),
        'lambda_q2': nrm(ks[5], (DEPTH, DIFF_D), 0.1),
        'lambda_k2': nrm(ks[6], (DEPTH, DIFF_D), 0.1),
        'diff_subln': gain(ks[7], (DEPTH, 2 * DIFF_D)),
        'cmp_pos_k': nrm(ks[8], (DEPTH, CMP_LEN, NSA_DK), 0.1),
        'cmp_pos_v': nrm(ks[9], (DEPTH, CMP_LEN, NSA_DV), 0.1),
        'cmp_k_w1': nrm(ks[10], (DEPTH, CMP_LEN * NSA_DK, CMP_HIDDEN), (CMP_LEN * NSA_DK) ** -0.5),
        'cmp_k_w2': nrm(ks[11], (DEPTH, CMP_HIDDEN, NSA_DK), CMP_HIDDEN ** -0.5),
        'cmp_v_w1': nrm(ks[12], (DEPTH, CMP_LEN * NSA_DV, CMP_HIDDEN), (CMP_LEN * NSA_DV) ** -0.5),
        'cmp_v_w2': nrm(ks[13], (DEPTH, CMP_HIDDEN, NSA_DV), CMP_HIDDEN ** -0.5),
        'rel_bias': nrm(ks[14], (N_BUCKETS, DIFF_HEADS + NSA_HEADS), 0.5),
        'w_out': nrm(ks[15], (DEPTH, MIX_WIDTH, D_MODEL), MIX_WIDTH ** -0.5),
        'post_mix_norm': gain(ks[16], (DEPTH, D_MODEL)),
        'pre_ffn_norm': gain(ks[17], (DEPTH, D_MODEL)),
        'w_up': nrm(ks[18], (DEPTH, D_MODEL, 2 * D_FF), D_MODEL ** -0.5),
        'conv_w': nrm(ks[19], (DEPTH, CONV_W, 2 * D_FF), CONV_W ** -0.5),
        'conv_b': nrm(ks[20], (DEPTH, 2 * D_FF), 0.01),
        'w_down': nrm(ks[21], (DEPTH, D_FF, D_MODEL), D_FF ** -0.5),
        'post_ffn_norm': gain(ks[22], (DEPTH, D_MODEL)),
    }


def reference(x, pre_mix_norm, w_in, lambda_q1, lambda_k1, lambda_q2, lambda_k2, diff_subln,
              cmp_pos_k, cmp_pos_v, cmp_k_w1, cmp_k_w2, cmp_v_w1, cmp_v_w2, rel_bias, w_out,
              post_mix_norm, pre_ffn_norm, w_up, conv_w, conv_b, w_down, post_ffn_norm):
    B, S, _ = x.shape
    f32 = jnp.float32
    offsets = np.cumsum(SPLIT_SIZES)[:-1].tolist()
    bias_diff = rel_bias[:, :DIFF_HEADS]
    bias_nsa = rel_bias[:, DIFF_HEADS:].reshape(N_BUCKETS, NSA_KV, NSA_GROUP)
    for l in range(DEPTH):
        h = rms_norm(x, pre_mix_norm[l])
        proj = h @ w_in[l]
        (dq, dk, dv, nq, kc, vc, ksl, vsl, kwn, vwn, g) = jnp.split(proj, offsets, axis=-1)
        q_d = dq.reshape(B, S, DIFF_HEADS, 2, DIFF_D).transpose(0, 2, 1, 3, 4)
        k_d = dk.reshape(B, S, DIFF_HEADS, 2, DIFF_D).transpose(0, 2, 1, 3, 4)
        v_d = dv.reshape(B, S, DIFF_HEADS, 2 * DIFF_D).transpose(0, 2, 1, 3)
        lam_init = 0.8 - 0.6 * math.exp(-0.3 * l)
        lam = (jnp.exp(jnp.sum(lambda_q1[l].astype(f32) * lambda_k1[l].astype(f32)))
               - jnp.exp(jnp.sum(lambda_q2[l].astype(f32) * lambda_k2[l].astype(f32))) + lam_init)
        o_diff = diff_attention(q_d, k_d, v_d, lam, lam_init, diff_subln[l], bias_diff)
        q_n = nq.reshape(B, S, NSA_KV, NSA_GROUP, NSA_DK).transpose(0, 2, 3, 1, 4)
        k_c = compress(kc.reshape(B, S, NSA_KV, NSA_DK), cmp_pos_k[l], cmp_k_w1[l], cmp_k_w2[l])
        v_c = compress(vc.reshape(B, S, NSA_KV, NSA_DV), cmp_pos_v[l], cmp_v_w1[l], cmp_v_w2[l])
        k_s = ksl.reshape(B, S, NSA_KV, NSA_DK).transpose(0, 2, 1, 3)
        v_s = vsl.reshape(B, S, NSA_KV, NSA_DV).transpose(0, 2, 1, 3)
        k_w = kwn.reshape(B, S, NSA_KV, NSA_DK).transpose(0, 2, 1, 3)
        v_w = vwn.reshape(B, S, NSA_KV, NSA_DV).transpose(0, 2, 1, 3)
        gates = jax.nn.sigmoid(g.reshape(B, S, NSA_KV, NSA_GROUP, 3)).transpose(0, 2, 3, 1, 4)
        o_nsa = nsa_attention(q_n, k_c, v_c, k_s, v_s, k_w, v_w, gates, bias_nsa)
        mix = jnp.concatenate([o_diff.transpose(0, 2, 1, 3).reshape(B, S, -1),
                               o_nsa.transpose(0, 3, 1, 2, 4).reshape(B, S, -1)], axis=-1)
        x = x + rms_norm(mix @ w_out[l], post_mix_norm[l])
        h = rms_norm(x, pre_ffn_norm[l])
        u = causal_dwconv(h @ w_up[l], conv_w[l], conv_b[l])
        gate, up = jnp.split(u, 2, axis=-1)
        y = (jax.nn.gelu(gate, approximate=True) * up) @ w_down[l]
        x = x + rms_norm(y, post_ffn_norm[l])
    return x
```

```python
import numpy as np
import ml_dtypes
from contextlib import ExitStack
import concourse.bass as bass
import concourse.mybir as mybir
from concourse.bass_utils import run_bass_kernel_spmd

F32 = mybir.dt.float32
BF16 = mybir.dt.bfloat16
AF = mybir.ActivationFunctionType
ALU = mybir.AluOpType

D = 2048
S = 8192
DFF = 5632
EPS = 1e-6
NCORES = 8


class Sem:
    def __init__(self, fw, name):
        self.h = fw.es.enter_context(fw.nc.semaphore(name))
        self.name = name
        self.val = 0


class Eng:
    def __init__(self, fw, eng, name, selfwait=True):
        self.eng = eng
        self.name = name
        self.sem = Sem(fw, "se_" + name)
        self.waited = {}
        self.selfwait = selfwait

    def wait(self, tok):
        if tok is None:
            return
        sem, v = tok
        if sem is self.sem and not self.selfwait:
            return
        if self.waited.get(sem.name, 0) >= v:
            return
        self.eng.wait_ge(sem.h, v)
        self.waited[sem.name] = v


class Buf:
    def __init__(self, t, name):
        self.t = t
        self.name = name
        self.w = None
        self.r = {}
        self.dsem = None

    def __getitem__(self, idx):
        return self.t[idx]


class FW:
    def __init__(self, nc, es):
        self.nc = nc
        self.es = es
        self.PE = Eng(self, nc.tensor, "pe", selfwait=False)
        self.ACT = Eng(self, nc.scalar, "act")
        self.DVE = Eng(self, nc.vector, "dve")
        self.POOL = Eng(self, nc.gpsimd, "pool")
        self.SP = Eng(self, nc.sync, "sp")
        self.engs = [self.PE, self.ACT, self.DVE, self.POOL, self.SP]
        self.dsems = []
        self.nbuf = 0

    def sbuf(self, es, name, shape, dtype):
        self.nbuf += 1
        name = "%s_%d" % (name, self.nbuf)
        t = es.enter_context(self.nc.sbuf_tensor(name, list(shape), dtype))
        return Buf(t, name)

    def psum(self, es, name, shape, dtype=F32):
        t = es.enter_context(self.nc.psum_tensor(name, list(shape), dtype))
        return Buf(t, name)

    def sub(self, ap, name="sub"):
        return Buf(ap, name)

    def dram(self, name, shape, dtype):
        t = self.nc.dram_tensor(name, list(shape), dtype, kind="Internal")
        return Buf(t.ap(), name)

    def op(self, E, fn, reads=(), writes=(), inc=True):
        for b in reads:
            E.wait(b.w)
        for b in writes:
            E.wait(b.w)
            for tok in b.r.values():
                E.wait(tok)
        ins = fn()
        if inc:
            E.sem.val += 1
            ins.then_inc(E.sem.h, 1)
            tok = (E.sem, E.sem.val)
        else:
            tok = (E.sem, E.sem.val + 1)
        for b in reads:
            b.r[E.name] = tok
        for b in writes:
            b.w = tok
            b.r = {}
        return tok

    def _dsem(self, b):
        if b.dsem is None:
            b.dsem = Sem(self, "sd%d" % len(self.dsems))
            self.dsems.append(b.dsem)
        return b.dsem

    def dma(self, Q, out_ap, in_ap, reads=(), writes=(), sem_buf=None, **kw):
        for b in reads:
            Q.wait(b.w)
        for b in writes:
            Q.wait(b.w)
            for tok in b.r.values():
                Q.wait(tok)
        sb = sem_buf if sem_buf is not None else (writes[0] if writes else reads[0])
        sem = self._dsem(sb)
        ins = Q.eng.dma_start(out=out_ap, in_=in_ap, **kw)
        sem.val += 16
        ins.then_inc(sem.h, 16)
        tok = (sem, sem.val)
        for b in reads:
            b.r["d_" + sem.name] = tok
        for b in writes:
            b.w = tok
            b.r = {}
        return tok

    def barrier(self):
        for E in self.engs:
            for E2 in self.engs:
                if E2 is not E and E2.sem.val > 0:
                    E.wait((E2.sem, E2.sem.val))
            for s in self.dsems:
                if s.val > 0:
                    E.wait((s, s.val))


def _mm(fw, out_buf, out_ap, lhsT_ap, rhs_ap, reads, start, stop, inc=None):
    if inc is None:
        inc = stop
    return fw.op(
        fw.PE,
        lambda: fw.nc.tensor.matmul(out_ap, lhsT=lhsT_ap, rhs=rhs_ap, start=start, stop=stop),
        reads=reads,
        writes=[out_buf] if start else [],
        inc=inc,
    ) if start else _mm_acc(fw, out_buf, out_ap, lhsT_ap, rhs_ap, reads, stop, inc)


def _mm_acc(fw, out_buf, out_ap, lhsT_ap, rhs_ap, reads, stop, inc):
    E = fw.PE
    for b in reads:
        E.wait(b.w)
    ins = fw.nc.tensor.matmul(out_ap, lhsT=lhsT_ap, rhs=rhs_ap, start=False, stop=stop)
    if inc:
        E.sem.val += 1
        ins.then_inc(E.sem.h, 1)
        tok = (E.sem, E.sem.val)
    else:
        tok = (E.sem, E.sem.val + 1)
    for b in reads:
        b.r[E.name] = tok
    out_buf.w = tok
    return tok


NTB = 2050


def build_B(nc, mixT=None, xT=None):
    if mixT is None:
        mixT = nc.dram_tensor("mixT", [D, NTB], BF16, kind="ExternalInput").ap()
    if xT is None:
        xT = nc.dram_tensor("xTo", [D, NTB], F32, kind="ExternalInput").ap()
    w_out = nc.dram_tensor("w_out", [D, D], F32, kind="ExternalInput").ap()
    w_up = nc.dram_tensor("w_up", [D, 2 * DFF], F32, kind="ExternalInput").ap()
    w_down = nc.dram_tensor("w_down", [DFF, D], F32, kind="ExternalInput").ap()
    gvec = nc.dram_tensor("gvec", [128, 48], F32, kind="ExternalInput").ap()
    convp = nc.dram_tensor("convp", [128, 4 * 88], F32, kind="ExternalInput").ap()
    outT = nc.dram_tensor("outT", [D, 2048], F32, kind="ExternalOutput").ap()

    with ExitStack() as es:
        fw = FW(nc, es)
        PE, ACT, DVE, POOL, SP = fw.PE, fw.ACT, fw.DVE, fw.POOL, fw.SP
        x1s = fw.dram("x1s", [D, NTB], F32)
        h2s = fw.dram("h2s", [D, NTB], BF16)
        ys = fw.dram("ys", [D, 2048], F32)

        gv = fw.sbuf(es, "gv", [128, 48], F32)
        cp = fw.sbuf(es, "cp", [128, 4 * 88], F32)
        ones = fw.sbuf(es, "ones", [128, 128], BF16)
        epsb = fw.sbuf(es, "epsb", [128, 1], F32)
        fw.dma(SP, gv[:], gvec[:, :], writes=[gv])
        fw.dma(SP, cp[:], convp[:, :], writes=[cp])
        fw.op(DVE, lambda: nc.vector.memset(ones[:], 1.0), writes=[ones])
        fw.op(DVE, lambda: nc.vector.memset(epsb[:], EPS), writes=[epsb])
        ps = [fw.psum(es, "ps%d" % i, [128, 512]) for i in range(8)]

        def rstd_from(ssq_ps, w, dst, tmp):
            fw.op(ACT, lambda: nc.scalar.activation(out=tmp[:, :w], in_=ssq_ps[:, :w], func=AF.Sqrt,
                                                    bias=epsb[:, 0:1], scale=1.0 / D),
                  reads=[ssq_ps, epsb], writes=[tmp])
            fw.op(DVE, lambda: nc.vector.reciprocal(out=dst[:, :w], in_=tmp[:, :w]), reads=[tmp], writes=[dst])

        with ExitStack() as e1:
            wo = fw.sbuf(e1, "wo", [128, 16, D], BF16)
            mx = [fw.sbuf(e1, "mx%d" % i, [128, 16, 512], BF16) for i in range(2)]
            xs = fw.sbuf(e1, "xs", [128, 16, 512], F32)
            a_t = e1.enter_context(nc.sbuf_tensor("a", [128, 16, 512], F32))
            ac = [Buf(a_t[:, mc, :], "a%d" % mc) for mc in range(16)]
            h2 = fw.sbuf(e1, "h2", [128, 16, 512], BF16)
            sq = [fw.sbuf(e1, "sq%d" % i, [128, 512], BF16) for i in range(4)]
            rs = fw.sbuf(e1, "rs", [128, 512], F32)
            rt = fw.sbuf(e1, "rt", [128, 512], F32)
            tmp = [fw.sbuf(e1, "tmp%d" % i, [128, 512], F32) for i in range(2)]
            for q in range(4):
                fw.dma(POOL, wo[:, q * 4:(q + 1) * 4, :],
                       w_out[q * 512:(q + 1) * 512, :].rearrange("(k p) n -> p k n", p=128), writes=[wo])
            subs = [(0, 2)] + [(2 + 512 * i, 512) for i in range(4)]

            def load_m(si):
                c0, w = subs[si]
                m = mx[si % 2]
                fw.dma(SP, m[:, :, :w], mixT[:, c0:c0 + w].rearrange("(k p) n -> p k n", p=128), writes=[m])
            load_m(0)
            for si, (c0, w) in enumerate(subs):
                m = mx[si % 2]
                fw.dma(SP, xs[:, :, :w], xT[:, c0:c0 + w].rearrange("(k p) n -> p k n", p=128), writes=[xs])
                if si + 1 < len(subs):
                    load_m(si + 1)
                pss = ps[6]
                for mc in range(16):
                    pa = ps[mc % 4]
                    for kc in range(16):
                        _mm(fw, pa, pa[:, :w], wo[:, kc, mc * 128:(mc + 1) * 128], m[:, kc, :w],
                            [wo, m], kc == 0, kc == 15)
                    fw.op(ACT, lambda: nc.scalar.copy(out=ac[mc][:, :w], in_=pa[:, :w]), reads=[pa], writes=[ac[mc]])
                    s_ = sq[mc % 4]
                    fw.op(DVE, lambda: nc.vector.tensor_tensor(out=s_[:, :w], in0=pa[:, :w], in1=ac[mc][:, :w], op=ALU.mult),
                          reads=[pa, ac[mc]], writes=[s_])
                    _mm(fw, pss, pss[:, :w], ones[:], s_[:, :w], [ones, s_], mc == 0, mc == 15, inc=True)
                rstd_from(pss, w, rs, rt)
                pss2 = ps[7]
                for mc in range(16):
                    t_ = tmp[mc % 2]
                    fw.op(DVE, lambda: nc.vector.scalar_tensor_tensor(out=t_[:, :w], in0=ac[mc][:, :w], scalar=gv[:, mc:mc + 1],
                                                                      in1=rs[:, :w], op0=ALU.mult, op1=ALU.mult),
                          reads=[ac[mc], gv, rs], writes=[t_])
                    fw.op(POOL, lambda: nc.gpsimd.tensor_tensor(out=ac[mc][:, :w], in0=t_[:, :w], in1=xs[:, mc, :w], op=ALU.add),
                          reads=[t_, xs], writes=[ac[mc]])
                    s_ = sq[mc % 4]
                    fw.op(ACT, lambda: nc.scalar.activation(out=s_[:, :w], in_=ac[mc][:, :w], func=AF.Square),
                          reads=[ac[mc]], writes=[s_])
                    _mm(fw, pss2, pss2[:, :w], ones[:], s_[:, :w], [ones, s_], mc == 0, mc == 15, inc=True)
                fw.dma(SP, x1s[:, c0:c0 + w].rearrange("(k p) n -> p k n", p=128), a_t[:, :, :w], reads=ac, writes=[x1s])
                rstd_from(pss2, w, rs, rt)
                for mc in range(16):
                    fw.op(DVE, lambda: nc.vector.scalar_tensor_tensor(out=h2[:, mc, :w], in0=ac[mc][:, :w],
                                                                      scalar=gv[:, 16 + mc:17 + mc], in1=rs[:, :w],
                                                                      op0=ALU.mult, op1=ALU.mult),
                          reads=[ac[mc], gv, rs], writes=[h2])
                fw.dma(SP, h2s[:, c0:c0 + w].rearrange("(k p) n -> p k n", p=128), h2[:, :, :w], reads=[h2], writes=[h2s])
            fw.barrier()

        with ExitStack() as e2:
            gT = fw.sbuf(e2, "gT", [128, 44, 1024], BF16)
            carry = fw.sbuf(e2, "carry", [128, 88, 2], F32)
            for ti in range(2):
                c0 = 0 if ti == 0 else 1026
                W = 1026 if ti == 0 else 1024
                with ExitStack() as e3:
                    h2t = fw.sbuf(e3, "h2t", [128, 16, 1026], BF16)
                    wu = [fw.sbuf(e3, "wu%d" % i, [128, 16, 256], BF16) for i in range(2)]
                    u = [[fw.sbuf(e3, "u%d_%d" % (i, p), [128, 1026], F32) for p in range(2)] for i in range(2)]
                    c = [[fw.sbuf(e3, "c%d_%d" % (i, p), [128, 1024], F32) for p in range(2)] for i in range(2)]
                    gl = [fw.sbuf(e3, "gl%d" % i, [128, 1024], F32) for i in range(2)]
                    fw.dma(SP, h2t[:, :, :W], h2s[:, c0:c0 + W].rearrange("(k p) n -> p k n", p=128),
                           reads=[h2s], writes=[h2t])
                    if ti == 0:
                        subs = [(0, 2, 0), (2, 512, 2), (514, 512, 514)]
                    else:
                        subs = [(0, 512, 2), (512, 512, 514)]

                    def load_w(i):
                        wb = wu[i % 2]
                        for p in range(2):
                            col = p * DFF + i * 128
                            fw.dma(POOL, wb[:, :, p * 128:(p + 1) * 128],
                                   w_up[:, col:col + 128].rearrange("(k p) n -> p k n", p=128), writes=[wb])
                    load_w(0)
                    for i in range(44):
                        if i + 1 < 44:
                            load_w(i + 1)
                        wb = wu[i % 2]
                        for p in range(2):
                            ub = u[i % 2][p]
                            cb = c[i % 2][p]
                            ch = p * 44 + i
                            if ti == 1:
                                fw.op(POOL, lambda: nc.gpsimd.tensor_copy(out=ub[:, 0:2], in_=carry[:, ch, :]),
                                      reads=[carry], writes=[ub])
                            for (hc, w, uc) in subs:
                                pa = ps[(2 * i + p) % 2 * 3 + (0 if w == 2 else (1 if uc == 2 else 2))]
                                for kc in range(16):
                                    _mm(fw, pa, pa[:, :w], wb[:, kc, p * 128:(p + 1) * 128], h2t[:, kc, hc:hc + w],
                                        [wb, h2t], kc == 0, kc == 15)
                                fw.op(ACT, lambda: nc.scalar.copy(out=ub[:, uc:uc + w], in_=pa[:, :w]), reads=[pa], writes=[ub])
                            if ti == 0:
                                fw.op(POOL, lambda: nc.gpsimd.tensor_copy(out=carry[:, ch, :], in_=ub[:, 1024:1026]),
                                      reads=[ub], writes=[carry])
                            fw.op(ACT, lambda: nc.scalar.activation(out=cb[:], in_=ub[:, 2:1026], func=AF.Identity,
                                                                    bias=cp[:, 264 + ch:265 + ch], scale=cp[:, 176 + ch:177 + ch]),
                                  reads=[ub, cp], writes=[cb])
                            fw.op(DVE, lambda: nc.vector.scalar_tensor_tensor(out=cb[:], in0=ub[:, 1:1025], scalar=cp[:, 88 + ch:89 + ch],
                                                                              in1=cb[:], op0=ALU.mult, op1=ALU.add),
                                  reads=[ub, cp, cb], writes=[cb])
                            fw.op(DVE, lambda: nc.vector.scalar_tensor_tensor(out=cb[:], in0=ub[:, 0:1024], scalar=cp[:, ch:ch + 1],
                                                                              in1=cb[:], op0=ALU.mult, op1=ALU.add),
                                  reads=[ub, cp, cb], writes=[cb])
                        g_ = gl[i % 2]
                        fw.op(ACT, lambda: nc.scalar.activation(out=g_[:], in_=c[i % 2][0][:], func=AF.Gelu_apprx_tanh),
                              reads=[c[i % 2][0]], writes=[g_])
                        fw.op(POOL, lambda: nc.gpsimd.tensor_tensor(out=gT[:, i, :], in0=g_[:], in1=c[i % 2][1][:], op=ALU.mult),
                              reads=[g_, c[i % 2][1]], writes=[gT])
                    fw.barrier()
                with ExitStack() as e4:
                    wd = [fw.sbuf(e4, "wd%d" % i, [128, 44, 128], BF16) for i in range(2)]
                    yb = [fw.sbuf(e4, "yb%d" % i, [128, 1024], F32) for i in range(2)]
                    sq = [fw.sbuf(e4, "sq2_%d" % i, [128, 1024], BF16) for i in range(2)]
                    rs = fw.sbuf(e4, "rs2", [128, 1024], F32)
                    rt = fw.sbuf(e4, "rt2", [128, 1024], F32)
                    yl = fw.sbuf(e4, "yl", [128, 16, 512], F32)
                    xl = fw.sbuf(e4, "xl", [128, 16, 512], F32)

                    def load_wd(mc):
                        wb = wd[mc % 2]
                        fw.dma(POOL, wb[:], w_down[:, mc * 128:(mc + 1) * 128].rearrange("(k p) n -> p k n", p=128),
                               writes=[wb])
                    load_wd(0)
                    pss = [ps[6], ps[7]]
                    for mc in range(16):
                        if mc + 1 < 16:
                            load_wd(mc + 1)
                        wb = wd[mc % 2]
                        y_ = yb[mc % 2]
                        s_ = sq[mc % 2]
                        for s in range(2):
                            pa = ps[(2 * mc + s) % 4]
                            for i in range(44):
                                _mm(fw, pa, pa[:, :], wb[:, i, :], gT[:, i, s * 512:(s + 1) * 512], [wb, gT], i == 0, i == 43)
                            fw.op(ACT, lambda: nc.scalar.copy(out=y_[:, s * 512:(s + 1) * 512], in_=pa[:, :]), reads=[pa], writes=[y_])
                            fw.op(DVE, lambda: nc.vector.tensor_tensor(out=s_[:, s * 512:(s + 1) * 512], in0=pa[:, :],
                                                                       in1=y_[:, s * 512:(s + 1) * 512],
                                                                       op=ALU.mult), reads=[pa, y_], writes=[s_])
                            _mm(fw, pss[s], pss[s][:, :], ones[:], s_[:, s * 512:(s + 1) * 512], [ones, s_], mc == 0, mc == 15, inc=True)
                        fw.dma(SP, ys[mc * 128:(mc + 1) * 128, ti * 1024:(ti + 1) * 1024], y_[:], reads=[y_], writes=[ys])
                    for s in range(2):
                        fw.op(ACT, lambda: nc.scalar.activation(out=rt[:, s * 512:(s + 1) * 512], in_=pss[s][:, :], func=AF.Sqrt,
                                                                bias=epsb[:, 0:1], scale=1.0 / D),
                              reads=[pss[s], epsb], writes=[rt])
                    fw.op(DVE, lambda: nc.vector.reciprocal(out=rs[:], in_=rt[:]), reads=[rt], writes=[rs])
                    for s in range(2):
                        t0 = ti * 1024 + s * 512
                        fw.dma(SP, yl[:], ys[:, t0:t0 + 512].rearrange("(k p) n -> p k n", p=128), reads=[ys], writes=[yl])
                        fw.dma(SP, xl[:], x1s[:, 2 + t0:2 + t0 + 512].rearrange("(k p) n -> p k n", p=128), reads=[x1s], writes=[xl])
                        for mc in range(16):
                            fw.op(DVE, lambda: nc.vector.scalar_tensor_tensor(out=yl[:, mc, :], in0=yl[:, mc, :],
                                                                              scalar=gv[:, 32 + mc:33 + mc],
                                                                              in1=rs[:, s * 512:(s + 1) * 512],
                                                                              op0=ALU.mult, op1=ALU.mult),
                                  reads=[yl, gv, rs], writes=[yl])
                            fw.op(POOL, lambda: nc.gpsimd.tensor_tensor(out=yl[:, mc, :], in0=yl[:, mc, :], in1=xl[:, mc, :], op=ALU.add),
                                  reads=[yl, xl], writes=[yl])
                        fw.dma(SP, outT[:, t0:t0 + 512].rearrange("(k p) n -> p k n", p=128), yl[:], reads=[yl], writes=[])
                    fw.barrier()
        fw.barrier()
    return nc


NQT = 16
BIGM = 30000.0
LD = 1151
LW = 1535
NF = 12
FW_COLS = NF * 128 + 6


class AState:
    pass


def _a_inputs(nc):
    I = AState()
    I.xT = nc.dram_tensor("xT", [D, S], F32, kind="ExternalInput").ap()
    I.wF = nc.dram_tensor("wF", [D, FW_COLS], F32, kind="ExternalInput").ap()
    I.wT = nc.dram_tensor("wT", [D, 512], F32, kind="ExternalInput").ap()
    I.g0 = nc.dram_tensor("g0", [128, 16], F32, kind="ExternalInput").ap()
    I.ohd = nc.dram_tensor("ohd", [33, LD], F32, kind="ExternalInput").ap()
    I.ohw = nc.dram_tensor("ohw", [33, LW], F32, kind="ExternalInput").ap()
    I.rb4 = nc.dram_tensor("rb4", [32, 4], F32, kind="ExternalInput").ap()
    I.lq = nc.dram_tensor("lq", [64, 4], F32, kind="ExternalInput").ap()
    I.subln = nc.dram_tensor("subln", [128, 1], F32, kind="ExternalInput").ap()
    I.ident = nc.dram_tensor("ident", [128, 128], F32, kind="ExternalInput").ap()
    return I


def build_A(nc, nsa=True, mixo=None):
    I = _a_inputs(nc)
    if mixo is None:
        mixo = nc.dram_tensor("mixo", [512, S], BF16, kind="ExternalOutput").ap()
    with ExitStack() as es:
        fw = FW(nc, es)
        PE, ACT, DVE, POOL, SP = fw.PE, fw.ACT, fw.DVE, fw.POOL, fw.SP
        QF = fw.dram("QF", [NF, 128, S], BF16)
        GT = fw.dram("GT", [6, S], F32)
        VT = fw.dram("VT", [S, 512], BF16)
        mixb = Buf(mixo, "mixo")

        ones = fw.sbuf(es, "ones", [128, 128], BF16)
        onesf = fw.sbuf(es, "onesf", [128, 128], F32)
        epsb = fw.sbuf(es, "epsb", [128, 1], F32)
        rbx = fw.sbuf(es, "rbx", [33, 4], F32)
        rbB = fw.sbuf(es, "rbB", [33, 4, 128], F32)
        b31 = fw.sbuf(es, "b31", [128, 4], F32)
        fw.op(DVE, lambda: nc.vector.memset(ones[:], 1.0), writes=[ones])
        fw.op(DVE, lambda: nc.vector.memset(onesf[:], 1.0), writes=[onesf])
        fw.op(DVE, lambda: nc.vector.memset(epsb[:], EPS), writes=[epsb])
        fw.op(DVE, lambda: nc.vector.memset(rbx[32:33, :], -BIGM), writes=[rbx])
        fw.dma(SP, rbx[0:32, :], I.rb4[:, :], writes=[rbx])
        for h in range(4):
            fw.op(DVE, lambda: nc.vector.tensor_copy(out=rbB[:, h, :], in_=rbx[:, h:h + 1].to_broadcast([33, 128])),
                  reads=[rbx], writes=[rbB])
        ps = [fw.psum(es, "ps%d" % i, [128, 512]) for i in range(8)]
        npat = [0]

        def make_pattern(h, kind, dst):
            L = LD if kind == "d" else LW
            W = L - 127
            src = I.ohd if kind == "d" else I.ohw
            npat[0] += 1
            Fd = fw.dram("Fd%d" % npat[0], [128, L], F32)
            with ExitStack() as ep:
                oh = fw.sbuf(ep, "oh", [33, L], F32)
                fbs = fw.sbuf(ep, "fbs", [128, L], F32)
                fw.dma(SP, oh[:], src[:, :], writes=[oh])
                for n0 in range(0, L, 512):
                    w = min(512, L - n0)
                    pa = ps[(n0 // 512) % 3]
                    _mm(fw, pa, pa[:, :w], rbB[:, h, :], oh[:, n0:n0 + w], [rbB, oh], True, True)
                    fw.op(ACT, lambda: nc.scalar.copy(out=fbs[:, n0:n0 + w], in_=pa[:, :w]), reads=[pa], writes=[fbs])
                if kind == "d":
                    fw.op(DVE, lambda: nc.vector.tensor_copy(out=b31[:, h:h + 1], in_=fbs[:, LD - 1:LD]),
                          reads=[fbs], writes=[b31])
                fw.dma(SP, Fd[:, :], fbs[:], reads=[fbs], writes=[Fd])
                skew = bass.AP(Fd.t.tensor, 127, [[L - 1, 128], [1, W]])
                fw.dma(SP, dst[:, :W], skew, reads=[Fd], writes=[dst])
                fw.barrier()

        with ExitStack() as e0:
            wFs = fw.sbuf(e0, "wFs", [128, 16, FW_COLS], BF16)
            wTs = fw.sbuf(e0, "wTs", [128, 16, 512], BF16)
            g0s = fw.sbuf(e0, "g0s", [128, 16], F32)
            xs = [fw.sbuf(e0, "xs%d" % i, [128, 16, 512], F32) for i in range(2)]
            hT_t = [e0.enter_context(nc.sbuf_tensor("hT%d" % i, [128, 16, 512], BF16)) for i in range(2)]
            hc = [[Buf(hT_t[i][:, kc, :], "hc") for kc in range(16)] for i in range(2)]
            fst_t = e0.enter_context(nc.sbuf_tensor("fst", [128, NF, 512], BF16))
            fst = [Buf(fst_t[:, fc, :], "fst") for fc in range(NF)]
            gst = fw.sbuf(e0, "gst", [6, 512], F32)
            vst_t = e0.enter_context(nc.sbuf_tensor("vst", [128, 4, 512], BF16))
            vst = [Buf(vst_t[:, tb, :], "vst") for tb in range(4)]
            sqb = [fw.sbuf(e0, "sqb%d" % i, [128, 512], BF16) for i in range(4)]
            rs = fw.sbuf(e0, "rs", [128, 512], F32)
            rt = fw.sbuf(e0, "rt", [128, 512], F32)
            fw.dma(SP, g0s[:], I.g0[:, :], writes=[g0s])
            for q in range(4):
                fw.dma(POOL, wFs[:, q * 4:(q + 1) * 4, :],
                       I.wF[q * 512:(q + 1) * 512, :].rearrange("(k p) n -> p k n", p=128), writes=[wFs])
            for q in range(2):
                fw.dma(POOL, wTs[:, q * 8:(q + 1) * 8, :],
                       I.wT[q * 1024:(q + 1) * 1024, :].rearrange("(k p) n -> p k n", p=128), writes=[wTs])

            def load_x(i):
                fw.dma(SP, xs[i % 2][:], I.xT[:, i * 512:(i + 1) * 512].rearrange("(k p) n -> p k n", p=128),
                       writes=[xs[i % 2]])
            load_x(0)
            for i in range(NQT):
                if i + 1 < NQT:
                    load_x(i + 1)
                x_ = xs[i % 2]
                pss = ps[7]
                for kc in range(16):
                    s_ = sqb[kc % 4]
                    fw.op(ACT, lambda: nc.scalar.activation(out=s_[:], in_=x_[:, kc, :], func=AF.Square),
                          reads=[x_], writes=[s_])
                    _mm(fw, pss, pss[:, :], ones[:], s_[:], [ones, s_], kc == 0, kc == 15, inc=True)
                fw.op(ACT, lambda: nc.scalar.activation(out=rt[:], in_=pss[:, :], func=AF.Sqrt, bias=epsb[:, 0:1], scale=1.0 / D),
                      reads=[pss, epsb], writes=[rt])
                fw.op(DVE, lambda: nc.vector.reciprocal(out=rs[:], in_=rt[:]), reads=[rt], writes=[rs])
                h_ = hc[i % 2]
                ht = hT_t[i % 2]
                for kc in range(16):
                    E = DVE
                    fw.op(E, lambda: E.eng.scalar_tensor_tensor(out=ht[:, kc, :], in0=x_[:, kc, :], scalar=g0s[:, kc:kc + 1],
                                                                in1=rs[:], op0=ALU.mult, op1=ALU.mult),
                          reads=[x_, g0s, rs], writes=[h_[kc]])
                for fc in range(NF + 1):
                    m = 128 if fc < NF else 6
                    pa = ps[fc % 4]
                    for kc in range(16):
                        _mm(fw, pa, pa[0:m, :], wFs[:, kc, fc * 128:fc * 128 + m], ht[:, kc, :], [wFs, h_[kc]],
                            kc == 0, kc == 15)
                    if fc < NF:
                        if fc % 2 == 0:
                            fw.op(ACT, lambda: nc.scalar.copy(out=fst_t[:, fc, :], in_=pa[:, :]), reads=[pa], writes=[fst[fc]])
                        else:
                            fw.op(DVE, lambda: nc.vector.tensor_copy(out=fst_t[:, fc, :], in_=pa[:, :]), reads=[pa], writes=[fst[fc]])
                    else:
                        fw.op(ACT, lambda: nc.scalar.activation(out=gst[:], in_=pa[0:6, :], func=AF.Sigmoid),
                              reads=[pa], writes=[gst])
                fw.dma(SP, QF[:, :, i * 512:(i + 1) * 512].rearrange("f p n -> p f n"), fst_t[:], reads=fst, writes=[QF])
                fw.dma(SP, GT[:, i * 512:(i + 1) * 512], gst[:], reads=[gst], writes=[GT])
                for tb in range(4):
                    pa = ps[4 + tb % 2]
                    for kc in range(16):
                        _mm(fw, pa, pa[:, :], ht[:, kc, tb * 128:(tb + 1) * 128], wTs[:, kc, :], [h_[kc], wTs],
                            kc == 0, kc == 15)
                    if tb % 2 == 0:
                        fw.op(ACT, lambda: nc.scalar.copy(out=vst_t[:, tb, :], in_=pa[:, :]), reads=[pa], writes=[vst[tb]])
                    else:
                        fw.op(DVE, lambda: nc.vector.tensor_copy(out=vst_t[:, tb, :], in_=pa[:, :]), reads=[pa], writes=[vst[tb]])
                fw.dma(SP, VT[i * 512:(i + 1) * 512, :].rearrange("(t p) c -> p t c", p=128), vst_t[:], reads=vst, writes=[VT])
            fw.barrier()

        with ExitStack() as e1:
            qT = [fw.sbuf(e1, "qT%d" % h, [128, S], BF16) for h in range(2)]
            kT = [fw.sbuf(e1, "kT%d" % h, [128, S], BF16) for h in range(2)]
            Vd = [fw.sbuf(e1, "Vd%d" % h, [128, 64, 128], BF16) for h in range(2)]
            Dd = [fw.sbuf(e1, "Dd%d" % h, [128, 1024], F32) for h in range(2)]
            for h in range(2):
                fw.dma(SP, qT[h][:], QF[h, :, :], reads=[QF], writes=[qT[h]])
                fw.dma(SP, kT[h][:], QF[2 + h, :, :], reads=[QF], writes=[kT[h]])
                fw.dma(SP, Vd[h][:], VT[:, h * 128:(h + 1) * 128].rearrange("(t p) c -> p t c", p=128),
                       reads=[VT], writes=[Vd[h]])
            for h in range(2):
                make_pattern(h, "d", Dd[h])
            lqs = fw.sbuf(e1, "lqs", [64, 4], F32)
            prod = fw.sbuf(e1, "prod", [64, 2], F32)
            ee = fw.sbuf(e1, "ee", [128, 2], F32)
            neglam = fw.sbuf(e1, "neglam", [128, 1], F32)
            subw = fw.sbuf(e1, "subw", [128, 1], F32)
            fw.dma(SP, lqs[:], I.lq[:, :], writes=[lqs])
            fw.dma(SP, subw[:], I.subln[:, :], writes=[subw])
            fw.op(DVE, lambda: nc.vector.tensor_tensor(out=prod[:, 0:1], in0=lqs[:, 0:1], in1=lqs[:, 1:2], op=ALU.mult),
                  reads=[lqs], writes=[prod])
            fw.op(DVE, lambda: nc.vector.tensor_tensor(out=prod[:, 1:2], in0=lqs[:, 2:3], in1=lqs[:, 3:4], op=ALU.mult),
                  reads=[lqs], writes=[prod])
            _mm(fw, ps[7], ps[7][:, 0:2], onesf[0:64, :], prod[:], [onesf, prod], True, True)
            fw.op(ACT, lambda: nc.scalar.activation(out=ee[:], in_=ps[7][:, 0:2], func=AF.Exp), reads=[ps[7]], writes=[ee])
            fw.op(DVE, lambda: nc.vector.tensor_tensor(out=neglam[:], in0=ee[:, 1:2], in1=ee[:, 0:1], op=ALU.subtract),
                  reads=[ee], writes=[neglam])
            fw.op(DVE, lambda: nc.vector.tensor_scalar_add(out=neglam[:], in0=neglam[:], scalar1=-0.2),
                  reads=[neglam], writes=[neglam])
            fw.op(DVE, lambda: nc.vector.tensor_scalar_mul(out=subw[:], in0=subw[:], scalar1=0.8), reads=[subw], writes=[subw])

            Pb = [fw.sbuf(e1, "Pb%d" % i, [128, 512], BF16) for i in range(3)]
            tmpd = [fw.sbuf(e1, "tmpd%d" % i, [128, 512], F32) for i in range(2)]
            On = [fw.sbuf(e1, "On%d" % i, [128, 512], F32) for i in range(2)]
            rden = fw.sbuf(e1, "rden", [128, 512], F32)
            ob = fw.sbuf(e1, "ob", [128, 512], F32)
            osq = fw.sbuf(e1, "osq", [128, 512], BF16)
            rr = fw.sbuf(e1, "rr", [128, 512], F32)
            rr2 = fw.sbuf(e1, "rr2", [128, 512], F32)
            ofin = [fw.sbuf(e1, "ofin%d" % i, [128, 512], BF16) for i in range(2)]
            sc = 64 ** -0.5
            cnt = 0
            for h in range(2):
                for i in range(NQT):
                    for m in range(2):
                        Ob = ps[3 + cnt % 2]
                        Db = ps[5 + cnt % 2]
                        cnt += 1
                        nk = 4 * i + 4
                        qs = qT[h][64 * m:64 * m + 64, i * 512:(i + 1) * 512]

                        def qk(kt):
                            Sb = ps[kt % 3]
                            _mm(fw, Sb, Sb[:, :], kT[h][64 * m:64 * m + 64, kt * 128:(kt + 1) * 128], qs, [kT[h], qT[h]], True, True)
                            P = Pb[kt % 3]
                            if kt >= 4 * i - 1:
                                off = 512 * i - 128 * kt + 384
                                t_ = tmpd[kt % 2]
                                fw.op(DVE, lambda: nc.vector.scalar_tensor_tensor(out=t_[:], in0=Sb[:, :], scalar=sc,
                                                                                  in1=Dd[h][:, off:off + 512],
                                                                                  op0=ALU.mult, op1=ALU.add),
                                      reads=[Sb, Dd[h]], writes=[t_])
                                fw.op(ACT, lambda: nc.scalar.activation(out=P[:], in_=t_[:], func=AF.Exp), reads=[t_], writes=[P])
                            else:
                                fw.op(ACT, lambda: nc.scalar.activation(out=P[:], in_=Sb[:, :], func=AF.Exp,
                                                                        bias=b31[:, h:h + 1], scale=sc),
                                      reads=[Sb, b31], writes=[P])

                        def pv(kt):
                            P = Pb[kt % 3]
                            _mm(fw, Ob, Ob[:, :], Vd[h][:, kt, :], P[:], [Vd[h], P], kt == 0, kt == nk - 1, inc=False)
                            _mm(fw, Db, Db[:, :], ones[:], P[:], [ones, P], kt == 0, kt == nk - 1, inc=True)
                        for st in range(nk + 2):
                            if st < nk:
                                qk(st)
                            if st >= 2:
                                pv(st - 2)
                        fw.op(DVE, lambda: nc.vector.reciprocal(out=rden[:], in_=Db[:, :]), reads=[Db], writes=[rden])
                        fw.op(DVE, lambda: nc.vector.tensor_tensor(out=On[m][:], in0=Ob[:, :], in1=rden[:], op=ALU.mult),
                              reads=[Ob, rden], writes=[On[m]])
                    fw.op(DVE, lambda: nc.vector.scalar_tensor_tensor(out=ob[:], in0=On[1][:], scalar=neglam[:, 0:1], in1=On[0][:],
                                                                      op0=ALU.mult, op1=ALU.add),
                          reads=[On[0], On[1], neglam], writes=[ob])
                    fw.op(ACT, lambda: nc.scalar.activation(out=osq[:], in_=ob[:], func=AF.Square), reads=[ob], writes=[osq])
                    _mm(fw, ps[7], ps[7][:, :], ones[:], osq[:], [ones, osq], True, True)
                    fw.op(ACT, lambda: nc.scalar.activation(out=rr[:], in_=ps[7][:, :], func=AF.Sqrt, bias=epsb[:, 0:1], scale=1.0 / 128),
                          reads=[ps[7], epsb], writes=[rr])
                    fw.op(DVE, lambda: nc.vector.reciprocal(out=rr2[:], in_=rr[:]), reads=[rr], writes=[rr2])
                    of = ofin[i % 2]
                    fw.op(DVE, lambda: nc.vector.scalar_tensor_tensor(out=of[:], in0=ob[:], scalar=subw[:, 0:1], in1=rr2[:],
                                                                      op0=ALU.mult, op1=ALU.mult),
                          reads=[ob, subw, rr2], writes=[of])
                    fw.dma(SP, mixo[h * 128:(h + 1) * 128, i * 512:(i + 1) * 512], of[:], reads=[of], writes=[mixb], sem_buf=of)
            fw.barrier()

        if nsa:
            build_A2(nc, fw, I, ps, QF, GT, VT, mixo, mixb, make_pattern, ones, onesf, epsb, b31)
        fw.barrier()
    return nc


def _rel_bucket_np(n):
    n = np.maximum(n, 0)
    nf = np.maximum(n, 1).astype(np.float32)
    large = 16 + (np.log(nf / np.float32(16)) / np.float32(np.log(8.0)) * np.float32(16)).astype(np.int32)
    large = np.minimum(large, 31)
    return np.where(n < 16, n, large)


def _consts():
    C = {}
    n = np.arange(LD)
    d = n - 511
    oh = np.zeros((33, LD), np.float32)
    bk = _rel_bucket_np(d)
    oh[np.where(d >= 0, bk, 32), n] = 1.0
    C["ohd"] = oh
    n = np.arange(LW)
    d = n - 511
    oh = np.zeros((33, LW), np.float32)
    bk = _rel_bucket_np(d)
    ok = (d >= 0) & (d < 512)
    oh[np.where(ok, bk, 32), n] = 1.0
    C["ohw"] = oh
    C["ident"] = np.eye(128, dtype=np.float32)
    key = np.arange(S)
    C["ebig"] = np.where((key[None, :] // 64) == np.arange(128)[:, None], BIGM, 0.0).astype(ml_dtypes.bfloat16)
    jj = np.arange(2560)
    C["mbig"] = ((jj[None, :] - 16 * np.arange(128)[:, None] - 31) >= 0).astype(np.float32).astype(ml_dtypes.bfloat16)
    sel = np.zeros((128, 512), np.float32)
    for j in range(128):
        for m_ in range(4):
            for n_ in range(2):
                c_ = 4 * j - m_ - n_
                if 0 <= c_ < 511:
                    sel[j, c_] += 1.0
    selx = np.zeros((128, 4, 129), np.float32)
    for cc in range(4):
        selx[:, cc, :128] = sel[:, cc * 128:(cc + 1) * 128].T
        selx[:, cc, 128] = 1.0
    C["selx"] = selx.reshape(128, 4 * 129).astype(ml_dtypes.bfloat16)
    ql = np.arange(128)[:, None]
    r = np.arange(256)[None, :] - 126
    cur = ql // 64
    aadd = np.zeros((128, 256), np.float32)
    aadd[np.broadcast_to(r > cur, aadd.shape)] = -1e30
    aadd[np.broadcast_to(r == cur, aadd.shape)] = 1e30
    aadd[np.broadcast_to(r == cur - 1, aadd.shape)] = 2e30
    C["aadd"] = aadd
    selg = np.zeros((6, 6, 128), np.float32)
    for r_ in range(6):
        selg[r_, r_, :] = 1.0
    C["selg"] = selg.reshape(6, 6 * 128)
    return C


def _pk(v, n):
    return np.ascontiguousarray(np.asarray(v, np.float32).reshape(n, 128).T)


def prep_A(inp, c, C):
    b, j = c // 4, c % 4
    g, hp = j // 2, j % 2
    w = inp["w_in"][0]
    hd = [2 * j, 2 * j + 1]
    cols = []
    for h in hd:
        cols.append(np.arange(h * 128, (h + 1) * 128))
    for h in hd:
        cols.append(1024 + np.arange(h * 128, (h + 1) * 128))
    for hh in (2 * hp, 2 * hp + 1, 2 * (1 - hp), 2 * (1 - hp) + 1):
        cols.append(3072 + (4 * g + hh) * 128 + np.arange(128))
    for off in (4096, 4352, 4608, 5120):
        cols.append(off + g * 128 + np.arange(128))
    for oh_ in range(2):
        cols.append(5632 + (4 * g + 2 * hp + oh_) * 3 + np.arange(3))
    cols = np.concatenate(cols)
    tcols = np.concatenate([2048 + hd[0] * 128 + np.arange(128), 2048 + hd[1] * 128 + np.arange(128),
                            4864 + g * 128 + np.arange(128), 5376 + g * 128 + np.arange(128)])
    rb = inp["rel_bias"]
    m = {
        "xT": np.ascontiguousarray(inp["x"][b].T),
        "wF": np.ascontiguousarray(w[:, cols]),
        "wT": np.ascontiguousarray(w[:, tcols]),
        "g0": _pk(inp["pre_mix_norm"][0], 16),
        "ohd": C["ohd"], "ohw": C["ohw"], "ident": C["ident"],
        "rb4": np.ascontiguousarray(rb[:, [2 * j, 2 * j + 1, 8 + 4 * g + 2 * hp, 8 + 4 * g + 2 * hp + 1]]),
        "lq": np.ascontiguousarray(np.stack([inp["lambda_q1"][0], inp["lambda_k1"][0],
                                             inp["lambda_q2"][0], inp["lambda_k2"][0]], axis=1)),
        "subln": np.ascontiguousarray(inp["diff_subln"][0].reshape(128, 1)),
        "w1k": np.ascontiguousarray(inp["cmp_k_w1"][0]), "w1v": np.ascontiguousarray(inp["cmp_v_w1"][0]),
        "w2k": np.ascontiguousarray(inp["cmp_k_w2"][0]), "w2v": np.ascontiguousarray(inp["cmp_v_w2"][0]),
        "posk": np.ascontiguousarray(inp["cmp_pos_k"][0].T), "posv": np.ascontiguousarray(inp["cmp_pos_v"][0].T),
        "ebig": C["ebig"], "mbig": C["mbig"], "selx": C["selx"], "aadd": C["aadd"], "selg": C["selg"],
    }
    return m


def _a2_inputs(nc):
    J = AState()
    J.w1k = nc.dram_tensor("w1k", [4096, 256], F32, kind="ExternalInput").ap()
    J.w1v = nc.dram_tensor("w1v", [4096, 256], F32, kind="ExternalInput").ap()
    J.w2k = nc.dram_tensor("w2k", [256, 128], F32, kind="ExternalInput").ap()
    J.w2v = nc.dram_tensor("w2v", [256, 128], F32, kind="ExternalInput").ap()
    J.posk = nc.dram_tensor("posk", [128, 32], F32, kind="ExternalInput").ap()
    J.posv = nc.dram_tensor("posv", [128, 32], F32, kind="ExternalInput").ap()
    J.ebig = nc.dram_tensor("ebig", [128, S], BF16, kind="ExternalInput").ap()
    J.mbig = nc.dram_tensor("mbig", [128, 2560], BF16, kind="ExternalInput").ap()
    J.selx = nc.dram_tensor("selx", [128, 4 * 129], BF16, kind="ExternalInput").ap()
    J.aadd = nc.dram_tensor("aadd", [128, 256], F32, kind="ExternalInput").ap()
    J.selg = nc.dram_tensor("selg", [6, 6 * 128], F32, kind="ExternalInput").ap()
    return J


def build_A2(nc, fw, I, ps, QF, GT, VT, mixo, mixb, make_pattern, ones, onesf, epsb, b31):
    J = _a2_inputs(nc)
    PE, ACT, DVE, POOL, SP = fw.PE, fw.ACT, fw.DVE, fw.POOL, fw.SP
    sc = 128 ** -0.5
    TINY = 1e-30
    with ExitStack() as e2:
        kcmpT = fw.sbuf(e2, "kcmpT", [128, 512], BF16)
        vcmp = fw.sbuf(e2, "vcmp", [128, 4, 128], BF16)
        with ExitStack() as ec:
            srcT = [fw.sbuf(ec, "kcT", [128, S], BF16), fw.sbuf(ec, "vcT", [128, S], BF16)]
            w1s = [fw.sbuf(ec, "w1ks", [128, 32, 256], BF16), fw.sbuf(ec, "w1vs", [128, 32, 256], BF16)]
            w2s = [fw.sbuf(ec, "w2ks", [128, 2, 128], BF16), fw.sbuf(ec, "w2vs", [128, 2, 128], BF16)]
            poss = [fw.sbuf(ec, "posks", [128, 32], BF16), fw.sbuf(ec, "posvs", [128, 32], BF16)]
            gh = [fw.sbuf(ec, "ghk", [128, 2, 512], BF16), fw.sbuf(ec, "ghv", [128, 2, 512], BF16)]
            posb = fw.sbuf(ec, "posb", [128, 4], F32)
            for wi, (w1, w2, pp) in enumerate([(J.w1k, J.w2k, J.posk), (J.w1v, J.w2v, J.posv)]):
                fw.dma(SP, srcT[wi][:], QF[8 + wi, :, :], reads=[QF], writes=[srcT[wi]])
                fw.dma(POOL, w1s[wi][:], w1.rearrange("(l p) n -> p l n", p=128), writes=[w1s[wi]])
                fw.dma(POOL, w2s[wi][:], w2.rearrange("(c p) n -> p c n", p=128), writes=[w2s[wi]])
                fw.dma(POOL, poss[wi][:], pp[:, :], writes=[poss[wi]])
                fw.op(DVE, lambda: nc.vector.memset(gh[wi][:], 0.0), writes=[gh[wi]])
            for wi in range(2):
                for hc in range(2):
                    pa = ps[hc]
                    for l in range(32):
                        _mm(fw, pa, pa[:, 0:511], w1s[wi][:, l, hc * 128:(hc + 1) * 128], srcT[wi][:, l:l + 16 * 510 + 1:16],
                            [w1s[wi], srcT[wi]], l == 0, l == 31)
                    pb = ps[2]
                    for l in range(32):
                        _mm(fw, pb, pb[:, 0:1], w1s[wi][:, l, hc * 128:(hc + 1) * 128], poss[wi][:, l:l + 1],
                            [w1s[wi], poss[wi]], l == 0, l == 31)
                    col = wi * 2 + hc
                    fw.op(ACT, lambda: nc.scalar.copy(out=posb[:, col:col + 1], in_=pb[:, 0:1]), reads=[pb], writes=[posb])
                    fw.op(ACT, lambda: nc.scalar.activation(out=gh[wi][:, hc, 0:511], in_=pa[:, 0:511], func=AF.Gelu_apprx_tanh,
                                                            bias=posb[:, col:col + 1], scale=1.0),
                          reads=[pa, posb], writes=[gh[wi]])
            pa = ps[3]
            for hc in range(2):
                _mm(fw, pa, pa[:, :], w2s[0][:, hc, :], gh[0][:, hc, :], [w2s[0], gh[0]], hc == 0, hc == 1)
            fw.op(ACT, lambda: nc.scalar.copy(out=kcmpT[:], in_=pa[:, :]), reads=[pa], writes=[kcmpT])
            for cc in range(4):
                pa = ps[4 + cc % 2]
                for hc in range(2):
                    _mm(fw, pa, pa[:, 0:128], gh[1][:, hc, cc * 128:(cc + 1) * 128], w2s[1][:, hc, :], [gh[1], w2s[1]],
                        hc == 0, hc == 1)
                fw.op(ACT, lambda: nc.scalar.copy(out=vcmp[:, cc, :], in_=pa[:, 0:128]), reads=[pa], writes=[vcmp])
            fw.barrier()

        nqT = [fw.sbuf(e2, "nqT%d" % h, [128, S], BF16) for h in range(4)]
        kslT = fw.sbuf(e2, "kslT", [128, S], BF16)
        vsl = fw.sbuf(e2, "vsl", [128, 64, 128], BF16)
        ebig = fw.sbuf(e2, "ebig", [128, S], BF16)
        mbig = fw.sbuf(e2, "mbig", [128, 2560], BF16)
        selx = fw.sbuf(e2, "selx", [128, 4 * 129], BF16)
        aadd = fw.sbuf(e2, "aadd", [128, 256], F32)
        selg = fw.sbuf(e2, "selg", [6, 6 * 128], F32)
        ident = fw.sbuf(e2, "ident", [128, 128], F32)
        Dd = [fw.sbuf(e2, "Ddn%d" % h, [128, 1024], F32) for h in range(2)]
        Dw = [fw.sbuf(e2, "Dwn%d" % h, [128, 1408], F32) for h in range(2)]
        for h in range(4):
            fw.dma(SP, nqT[h][:], QF[4 + h, :, :], reads=[QF], writes=[nqT[h]])
        fw.dma(SP, kslT[:], QF[10, :, :], reads=[QF], writes=[kslT])
        fw.dma(SP, vsl[:], VT[:, 256:384].rearrange("(t p) c -> p t c", p=128), reads=[VT], writes=[vsl])
        fw.dma(SP, ebig[:], J.ebig[:, :], writes=[ebig])
        fw.dma(SP, mbig[:], J.mbig[:, :], writes=[mbig])
        fw.dma(SP, selx[:], J.selx[:, :], writes=[selx])
        fw.dma(SP, aadd[:], J.aadd[:, :], writes=[aadd])
        fw.dma(SP, selg[:], J.selg[:, :], writes=[selg])
        fw.dma(SP, ident[:], I.ident[:, :], writes=[ident])
        for h in range(2):
            make_pattern(2 + h, "d", Dd[h])
            make_pattern(2 + h, "w", Dw[h])

        kw = [fw.sbuf(e2, "kw%d" % i, [128, 1024], BF16) for i in range(2)]
        vw = [fw.sbuf(e2, "vw%d" % i, [128, 8, 128], BF16) for i in range(2)]
        gsb = [fw.sbuf(e2, "gsb%d" % i, [6, 512], F32) for i in range(2)]
        Pc = [[fw.sbuf(e2, "Pc%d_%d" % (h, cc), [128, 512], BF16) for cc in range(4)] for h in range(4)]
        Pb = [fw.sbuf(e2, "Pbn%d" % i, [128, 512], BF16) for i in range(3)]
        tmpd = [fw.sbuf(e2, "tmpn%d" % i, [128, 512], F32) for i in range(2)]
        selT = fw.sbuf(e2, "selT", [128, 512], BF16)
        imp = fw.sbuf(e2, "imp", [128, 128], F32)
        imp2 = fw.sbuf(e2, "imp2", [128, 128], F32)
        selm = fw.sbuf(e2, "selm", [128, 128], F32)
        dq4 = fw.sbuf(e2, "dq4", [128, 4], F32)
        rq4 = fw.sbuf(e2, "rq4", [128, 4], F32)
        m1 = fw.sbuf(e2, "m1", [128, 8], F32)
        m2 = fw.sbuf(e2, "m2", [128, 8], F32)
        rden = fw.sbuf(e2, "rdenn", [128, 512], F32)
        rg = fw.sbuf(e2, "rg", [128, 512], F32)
        tt = fw.sbuf(e2, "tt", [128, 512], F32)
        acc = fw.sbuf(e2, "acc", [128, 512], F32)
        ofin = [fw.sbuf(e2, "ofn%d" % i, [128, 512], BF16) for i in range(2)]

        def load_win(i):
            lo = 512 * i - 512
            a0 = 0 if lo >= 0 else 512
            lo2 = max(lo, 0)
            n = 512 * i + 512 - lo2
            fw.dma(SP, kw[i % 2][:, a0:a0 + n], QF[11, :, lo2:lo2 + n], reads=[QF], writes=[kw[i % 2]])
            fw.dma(SP, vw[i % 2][:, a0 // 128:a0 // 128 + n // 128, :],
                   VT[lo2:lo2 + n, 384:512].rearrange("(t p) c -> p t c", p=128), reads=[VT], writes=[vw[i % 2]])
            fw.dma(SP, gsb[i % 2][:], GT[:, 512 * i:512 * i + 512], reads=[GT], writes=[gsb[i % 2]])

        Ob, Db, G = ps[3], ps[4], ps[7]
        psI = [ps[5], ps[6]]
        scount = [0]

        def run_branch(tiles, qk, pvl, vtile):
            n = len(tiles)
            slots = []
            for st in range(n + 2):
                if st < n:
                    k = scount[0] % 3
                    scount[0] += 1
                    slots.append(k)
                    qk(tiles[st], ps[k], Pb[k])
                if st >= 2:
                    j_ = st - 2
                    P = Pb[slots[j_]]
                    _mm(fw, Ob, Ob[:, :], vtile(tiles[j_]), P[:], pvl + [P], j_ == 0, j_ == n - 1, inc=False)
                    _mm(fw, Db, Db[:, :], ones[:], P[:], [ones, P], j_ == 0, j_ == n - 1, inc=True)

        def finalize(oh, branch, first, gs):
            fw.op(DVE, lambda: nc.vector.tensor_scalar_max(out=rden[:], in0=Db[:, :], scalar1=TINY), reads=[Db], writes=[rden])
            fw.op(DVE, lambda: nc.vector.reciprocal(out=rden[:], in_=rden[:]), reads=[rden], writes=[rden])
            r = oh * 3 + branch
            _mm(fw, G, G[:, :], selg[:, r * 128:(r + 1) * 128], gs[:], [selg, gs], True, True)
            fw.op(DVE, lambda: nc.vector.tensor_tensor(out=rg[:], in0=G[:, :], in1=rden[:], op=ALU.mult),
                  reads=[G, rden], writes=[rg])
            if first:
                fw.op(DVE, lambda: nc.vector.tensor_tensor(out=acc[:], in0=Ob[:, :], in1=rg[:], op=ALU.mult),
                      reads=[Ob, rg], writes=[acc])
            else:
                fw.op(DVE, lambda: nc.vector.tensor_tensor(out=tt[:], in0=Ob[:, :], in1=rg[:], op=ALU.mult),
                      reads=[Ob, rg], writes=[tt])
                fw.op(POOL, lambda: nc.gpsimd.tensor_tensor(out=acc[:], in0=acc[:], in1=tt[:], op=ALU.add),
                      reads=[acc, tt], writes=[acc])

        load_win(0)
        for i in range(NQT):
            if i + 1 < NQT:
                load_win(i + 1)
            qsl = slice(i * 512, (i + 1) * 512)
            ccmax = (512 * i + 480) // 2048
            for hh in range(4):
                for cc in range(ccmax + 1):
                    k = scount[0] % 3
                    scount[0] += 1
                    Sb = ps[k]
                    _mm(fw, Sb, Sb[:, :], kcmpT[:, cc * 128:(cc + 1) * 128], nqT[hh][:, qsl], [kcmpT, nqT[hh]], True, True)
                    P = Pc[hh][cc]
                    fw.op(ACT, lambda: nc.scalar.activation(out=P[:], in_=Sb[:, :], func=AF.Exp, scale=sc), reads=[Sb], writes=[P])
                    if i <= 4 * cc + 4:
                        off = 512 * i - 2048 * cc
                        fw.op(POOL, lambda: nc.gpsimd.tensor_tensor(out=P[:], in0=P[:], in1=mbig[:, off:off + 512], op=ALU.mult),
                              reads=[P, mbig], writes=[P])
            for qb in range(4):
                for hh in range(4):
                    bank = psI[hh // 2]
                    c0 = (hh % 2) * 129
                    for cc in range(ccmax + 1):
                        _mm(fw, bank, bank[:, c0:c0 + 129], Pc[hh][cc][:, qb * 128:(qb + 1) * 128], selx[:, cc * 129:(cc + 1) * 129],
                            [Pc[hh][cc], selx], cc == 0 and hh % 2 == 0, cc == ccmax, inc=(cc == ccmax))
                for hh in range(4):
                    bank = psI[hh // 2]
                    c0 = (hh % 2) * 129
                    fw.op(DVE, lambda: nc.vector.tensor_scalar_max(out=dq4[:, hh:hh + 1], in0=bank[:, c0 + 128:c0 + 129], scalar1=TINY),
                          reads=[bank], writes=[dq4])
                fw.op(DVE, lambda: nc.vector.reciprocal(out=rq4[:], in_=dq4[:]), reads=[dq4], writes=[rq4])
                for hh in range(4):
                    bank = psI[hh // 2]
                    c0 = (hh % 2) * 129
                    if hh == 0:
                        fw.op(DVE, lambda: nc.vector.tensor_scalar_mul(out=imp[:], in0=bank[:, c0:c0 + 128], scalar1=rq4[:, 0:1]),
                              reads=[bank, rq4], writes=[imp])
                    else:
                        fw.op(DVE, lambda: nc.vector.scalar_tensor_tensor(out=imp[:], in0=bank[:, c0:c0 + 128], scalar=rq4[:, hh:hh + 1],
                                                                          in1=imp[:], op0=ALU.mult, op1=ALU.add),
                              reads=[bank, rq4, imp], writes=[imp])
                js = 126 - 8 * i - 2 * qb
                fw.op(DVE, lambda: nc.vector.tensor_tensor(out=imp[:], in0=imp[:], in1=aadd[:, js:js + 128], op=ALU.add),
                      reads=[imp, aadd], writes=[imp])
                fw.op(DVE, lambda: nc.vector.memset(imp[:, 0:1], 3e30), reads=[], writes=[imp])
                fw.op(DVE, lambda: nc.vector.max(out=m1[:], in_=imp[:]), reads=[imp], writes=[m1])
                fw.op(DVE, lambda: nc.vector.match_replace(out=imp2[:], in_to_replace=m1[:], in_values=imp[:], imm_value=-3e38),
                      reads=[m1, imp], writes=[imp2])
                fw.op(DVE, lambda: nc.vector.max(out=m2[:], in_=imp2[:]), reads=[imp2], writes=[m2])
                fw.op(DVE, lambda: nc.vector.tensor_scalar(out=selm[:], in0=imp[:], scalar1=m2[:, 7:8], scalar2=1.0,
                                                           op0=ALU.is_ge, op1=ALU.subtract),
                      reads=[imp, m2], writes=[selm])
                fw.op(PE, lambda: nc.tensor.transpose(out=G[:, qb * 128:(qb + 1) * 128], in_=selm[:], identity=ident[:]),
                      reads=[selm, ident], writes=[G])
                fw.op(ACT, lambda: nc.scalar.copy(out=selT[:, qb * 128:(qb + 1) * 128], in_=G[:, qb * 128:(qb + 1) * 128]),
                      reads=[G], writes=[selT])
            kwb, vwb, gs = kw[i % 2], vw[i % 2], gsb[i % 2]
            for oh in range(2):
                for cc in range(ccmax + 1):
                    P = Pc[oh][cc]
                    _mm(fw, Ob, Ob[:, :], vcmp[:, cc, :], P[:], [vcmp, P], cc == 0, cc == ccmax, inc=False)
                    _mm(fw, Db, Db[:, :], ones[:], P[:], [ones, P], cc == 0, cc == ccmax, inc=True)
                finalize(oh, 0, True, gs)

                def qk_win(kr, Sb, P):
                    _mm(fw, Sb, Sb[:, :], kwb[:, kr * 128:(kr + 1) * 128], nqT[oh][:, qsl], [kwb, nqT[oh]], True, True)
                    off = 896 - 128 * kr
                    t_ = tmpd[kr % 2]
                    fw.op(DVE, lambda: nc.vector.scalar_tensor_tensor(out=t_[:], in0=Sb[:, :], scalar=sc, in1=Dw[oh][:, off:off + 512],
                                                                      op0=ALU.mult, op1=ALU.add),
                          reads=[Sb, Dw[oh]], writes=[t_])
                    fw.op(ACT, lambda: nc.scalar.activation(out=P[:], in_=t_[:], func=AF.Exp), reads=[t_], writes=[P])
                wt = [kr for kr in range(8) if 4 * i - 4 + kr >= 0]
                run_branch(wt, qk_win, [vwb], lambda kr: vwb[:, kr, :])
                finalize(oh, 2, False, gs)

                def qk_slc(kt, Sb, P):
                    _mm(fw, Sb, Sb[:, :], kslT[:, kt * 128:(kt + 1) * 128], nqT[oh][:, qsl], [kslT, nqT[oh]], True, False, inc=False)
                    _mm(fw, Sb, Sb[:, :], ebig[:, kt * 128:(kt + 1) * 128], selT[:], [ebig, selT], False, True, inc=True)
                    if kt >= 4 * i - 1:
                        off = 512 * i - 128 * kt + 384
                        t_ = tmpd[kt % 2]
                        fw.op(DVE, lambda: nc.vector.scalar_tensor_tensor(out=t_[:], in0=Sb[:, :], scalar=sc, in1=Dd[oh][:, off:off + 512],
                                                                          op0=ALU.mult, op1=ALU.add),
                              reads=[Sb, Dd[oh]], writes=[t_])
                        fw.op(ACT, lambda: nc.scalar.activation(out=P[:], in_=t_[:], func=AF.Exp), reads=[t_], writes=[P])
                    else:
                        fw.op(ACT, lambda: nc.scalar.activation(out=P[:], in_=Sb[:, :], func=AF.Exp, bias=b31[:, 2 + oh:3 + oh], scale=sc),
                              reads=[Sb, b31], writes=[P])
                run_branch(list(range(4 * i + 4)), qk_slc, [vsl], lambda kt: vsl[:, kt, :])
                finalize(oh, 1, False, gs)
                of = ofin[oh]
                fw.op(ACT, lambda: nc.scalar.copy(out=of[:], in_=acc[:]), reads=[acc], writes=[of])
                fw.dma(SP, mixo[256 + oh * 128:256 + (oh + 1) * 128, qsl], of[:], reads=[of], writes=[mixb], sem_buf=of)
        fw.barrier()


def prep_B(inp, b, jq, mixT_b, C):
    lo = 2048 * jq - 2
    mt = np.zeros((D, NTB), ml_dtypes.bfloat16)
    xt = np.zeros((D, NTB), np.float32)
    xTb = inp["x"][b].T
    if lo >= 0:
        mt[:, :] = mixT_b[:, lo:lo + NTB]
        xt[:, :] = xTb[:, lo:lo + NTB]
    else:
        mt[:, 2:] = mixT_b[:, 0:2048]
        xt[:, 2:] = xTb[:, 0:2048]
    cw, cb = inp["conv_w"][0], inp["conv_b"][0]
    return {
        "mixT": mt, "xTo": xt,
        "w_out": np.ascontiguousarray(inp["w_out"][0]),
        "w_up": np.ascontiguousarray(inp["w_up"][0]),
        "w_down": np.ascontiguousarray(inp["w_down"][0]),
        "gvec": np.ascontiguousarray(np.concatenate([_pk(inp["post_mix_norm"][0], 16), _pk(inp["pre_ffn_norm"][0], 16),
                                                     _pk(inp["post_ffn_norm"][0], 16)], axis=1)),
        "convp": np.ascontiguousarray(np.concatenate([_pk(cw[0], 88), _pk(cw[1], 88), _pk(cw[2], 88), _pk(cb, 88)], axis=1)),
    }


def kernel(**inputs):
    inp = {k: np.asarray(v) for k, v in inputs.items()}
    C = _consts()
    ncA = bass.Bass("TRN2", target_bir_lowering=False)
    build_A(ncA, nsa=True)
    resA = run_bass_kernel_spmd(ncA, [prep_A(inp, c, C) for c in range(NCORES)], core_ids=list(range(NCORES)))
    mixT = [np.zeros((D, S), ml_dtypes.bfloat16) for _ in range(2)]
    for c in range(NCORES):
        b, j = c // 4, c % 4
        mo = np.asarray(resA.results[c]["mixo"])
        mixT[b][256 * j:256 * j + 256, :] = mo[0:256]
        mixT[b][1024 + 256 * j:1024 + 256 * j + 256, :] = mo[256:512]
    ncB = bass.Bass("TRN2", target_bir_lowering=False)
    build_B(ncB)
    resB = run_bass_kernel_spmd(ncB, [prep_B(inp, c // 4, c % 4, mixT[c // 4], C) for c in range(NCORES)],
                                core_ids=list(range(NCORES)))
    out = np.zeros((2, S, D), np.float32)
    for c in range(NCORES):
        b, j = c // 4, c % 4
        out[b, 2048 * j:2048 * (j + 1), :] = np.asarray(resB.results[c]["outT"]).T
    return out
```
